# Optimizing a Trainium2 kernel written in Bass

```python
import jax, jax.numpy as jnp
from jax import lax
import numpy as np

D_MODEL = 1024
BATCH = 4
SEQ = 8192
DEPTH = 2

HEAD_DIM = 64
H_GDN = (3 * D_MODEL) // (8 * HEAD_DIM)
H_MOBA = D_MODEL // (4 * HEAD_DIM)
H_MLSTM = D_MODEL // HEAD_DIM - H_GDN - H_MOBA
W_GDN = H_GDN * HEAD_DIM
W_MOBA = H_MOBA * HEAD_DIM
W_MLSTM = H_MLSTM * HEAD_DIM
MIX_WIDTH = W_GDN + W_MOBA + W_MLSTM
CONV_WIDTH = 4
GDN_CHUNK = 64
MLSTM_CHUNK = 64
MOBA_BLOCK = 256
MOBA_TOPK = 3
MOBA_Q_BLOCK = 128
D_FF = 2816
N_SUB = 3
ALPHA = (2 * DEPTH) ** 0.25
BETA_INIT = (8 * DEPTH) ** -0.25
ADA_INIT = 0.1
LN_EPS = 1e-5
NORM_EPS = 1e-6

IN_SIZES = [3 * W_GDN, H_GDN, H_GDN, W_GDN,
            3 * W_MOBA,
            2 * W_MLSTM, W_MLSTM, H_MLSTM, H_MLSTM, W_MLSTM]
IN_SPLITS = [int(s) for s in np.cumsum(IN_SIZES)[:-1]]
D_IN = int(sum(IN_SIZES))

kernel_name = "hymba_gdn_moba_mlstm_macaron_deepnorm"


def layer_norm(x, g, b):
    xf = x.astype(jnp.float32)
    mu = jnp.mean(xf, -1, keepdims=True)
    var = jnp.mean(jnp.square(xf - mu), -1, keepdims=True)
    return ((xf - mu) * lax.rsqrt(var + LN_EPS) * g + b).astype(x.dtype)


def modulate(x, shift, scale):
    return x * (1 + scale[:, None, :]) + shift[:, None, :]


def swiglu(h, w13, w2):
    a, b = jnp.split(h @ w13, 2, axis=-1)
    return (jax.nn.silu(a) * b) @ w2


def causal_conv_silu(x, w):
    y = lax.conv_general_dilated(x, w[:, None, :], window_strides=(1,), padding=[(CONV_WIDTH - 1, 0)],
                                 dimension_numbers=('NWC', 'WIO', 'NWC'), feature_group_count=x.shape[-1])
    return jax.nn.silu(y)


def split_heads(t):
    b_, t_, w = t.shape
    return t.reshape(b_, t_, w // HEAD_DIM, HEAD_DIM).transpose(0, 2, 1, 3)


def l2norm(x):
    return x * lax.rsqrt(jnp.sum(x * x, -1, keepdims=True) + NORM_EPS)


def to_chunks(a, size):
    b_, h_, t_ = a.shape[:3]
    return jnp.moveaxis(a.reshape(b_, h_, t_ // size, size, *a.shape[3:]), 2, 0)


def from_chunks(a):
    n, b_, h_, l, d = a.shape
    return jnp.moveaxis(a, 0, 2).reshape(b_, h_, n * l, d)


def gated_delta_rule(q, k, v, g, beta):
    b_, h_, t_, dk = q.shape
    dv = v.shape[-1]
    L = GDN_CHUNK
    q = to_chunks(q * dk ** -0.5, L); k = to_chunks(k, L); v = to_chunks(v, L)
    g = to_chunks(g, L); beta = to_chunks(beta, L)
    G = jnp.cumsum(g, axis=-1)
    incl = jnp.tril(jnp.ones((L, L), dtype=bool))
    strict = jnp.tril(jnp.ones((L, L), dtype=bool), -1)
    decay = jnp.exp(jnp.where(incl, G[..., :, None] - G[..., None, :], -jnp.inf))
    kb = k * beta[..., None]
    A = jnp.where(strict, jnp.einsum('nbhid,nbhjd->nbhij', kb, k) * decay, 0.0)
    M = A + jnp.eye(L, dtype=A.dtype)
    u = lax.linalg.triangular_solve(M, v * beta[..., None], left_side=True, lower=True, unit_diagonal=True)
    w = lax.linalg.triangular_solve(M, kb * jnp.exp(G)[..., None], left_side=True, lower=True, unit_diagonal=True)
    attn = jnp.einsum('nbhid,nbhjd->nbhij', q, k) * decay
    q_dec = q * jnp.exp(G)[..., None]
    k_dec = k * jnp.exp(G[..., -1:] - G)[..., None]
    g_last = jnp.exp(G[..., -1])

    def step(S, inp):
        q_i, w_i, u_i, k_i, a_i, gl_i = inp
        v_new = u_i - jnp.einsum('bhld,bhde->bhle', w_i, S)
        o = jnp.einsum('bhld,bhde->bhle', q_i, S) + jnp.einsum('bhls,bhse->bhle', a_i, v_new)
        S = S * gl_i[..., None, None] + jnp.einsum('bhld,bhle->bhde', k_i, v_new)
        return S, o

    S0 = jnp.zeros((b_, h_, dk, dv), q.dtype)
    _, o = lax.scan(step, S0, (q_dec, w, u, k_dec, attn, g_last))
    return from_chunks(o)


def moba_attention(q, k, v):
    b_, h_, t_, d = q.shape
    n_blk = -(-t_ // MOBA_BLOCK)
    top_k = min(MOBA_TOPK, n_blk)
    pad = n_blk * MOBA_BLOCK - t_
    kp = jnp.pad(k, ((0, 0), (0, 0), (0, pad), (0, 0)))
    vp = jnp.pad(v, ((0, 0), (0, 0), (0, pad), (0, 0)))
    k_blk = kp.reshape(b_, h_, n_blk, MOBA_BLOCK, d)
    v_blk = vp.reshape(b_, h_, n_blk, MOBA_BLOCK, d)
    k_mean = jnp.mean(k_blk, axis=3)
    scale = d ** -0.5
    bi = jnp.arange(b_)[:, None, None, None]
    hi = jnp.arange(h_)[None, :, None, None]
    blk_ids = jnp.arange(n_blk)
    n_sel = top_k * MOBA_BLOCK

    def one_block(qb):
        q0 = qb * MOBA_Q_BLOCK
        own = q0 // MOBA_BLOCK
        qc = lax.dynamic_slice_in_dim(q, q0, MOBA_Q_BLOCK, axis=2)
        gate = jnp.einsum('bhqd,bhnd->bhqn', qc, k_mean)
        gate = jnp.where(blk_ids < own, gate, -jnp.inf)
        _, sel = lax.top_k(gate, top_k)
        sel_ok = jnp.repeat(jnp.arange(top_k) < own, MOBA_BLOCK)
        k_sel = k_blk[bi, hi, sel].reshape(b_, h_, MOBA_Q_BLOCK, n_sel, d)
        v_sel = v_blk[bi, hi, sel].reshape(b_, h_, MOBA_Q_BLOCK, n_sel, d)
        s_sel = jnp.where(sel_ok, jnp.einsum('bhqd,bhqkd->bhqk', qc, k_sel) * scale, -jnp.inf)
        k_own = lax.dynamic_slice_in_dim(kp, own * MOBA_BLOCK, MOBA_BLOCK, axis=2)
        v_own = lax.dynamic_slice_in_dim(vp, own * MOBA_BLOCK, MOBA_BLOCK, axis=2)
        q_pos = q0 + jnp.arange(MOBA_Q_BLOCK)
        k_pos = own * MOBA_BLOCK + jnp.arange(MOBA_BLOCK)
        s_own = jnp.where(k_pos[None, :] <= q_pos[:, None],
                          jnp.einsum('bhqd,bhkd->bhqk', qc, k_own) * scale, -jnp.inf)
        p = jax.nn.softmax(jnp.concatenate([s_sel, s_own], -1).astype(jnp.float32), axis=-1).astype(q.dtype)
        return (jnp.einsum('bhqk,bhqkd->bhqd', p[..., :n_sel], v_sel)
                + jnp.einsum('bhqk,bhkd->bhqd', p[..., n_sel:], v_own))

    out = lax.map(one_block, jnp.arange(t_ // MOBA_Q_BLOCK))
    return from_chunks(out)


def mlstm_chunkwise(q, k, v, i_pre, logf):
    b_, h_, t_, d = q.shape
    L = MLSTM_CHUNK
    q = to_chunks(q, L); k = to_chunks(k * d ** -0.5, L); v = to_chunks(v, L)
    i_pre = to_chunks(i_pre, L); logf = to_chunks(logf, L)
    bcum = jnp.cumsum(logf, axis=-1)
    incl = jnp.tril(jnp.ones((L, L), dtype=bool))
    D = jnp.where(incl, bcum[..., :, None] - bcum[..., None, :] + i_pre[..., None, :], -jnp.inf)
    D_max = jnp.max(D, axis=-1)
    w_end = bcum[..., -1:] - bcum + i_pre
    qk = jnp.einsum('nbhtd,nbhsd->nbhts', q, k)

    def step(carry, inp):
        C, n, m = carry
        q_i, k_i, v_i, b_i, D_i, Dm_i, qk_i, we_i = inp
        inter = b_i + m[..., None]
        m_t = jnp.maximum(inter, Dm_i)
        a_inter = jnp.exp(inter - m_t)
        P = qk_i * jnp.exp(D_i - m_t[..., None])
        num = a_inter[..., None] * jnp.einsum('bhld,bhde->bhle', q_i, C) + jnp.einsum('bhls,bhse->bhle', P, v_i)
        den = a_inter * jnp.einsum('bhld,bhd->bhl', q_i, n) + jnp.sum(P, -1)
        h = num / jnp.maximum(jnp.abs(den), jnp.exp(-m_t))[..., None]
        inter_end = b_i[..., -1] + m
        m_new = jnp.maximum(inter_end, jnp.max(we_i, -1))
        s = jnp.exp(we_i - m_new[..., None])
        decay = jnp.exp(inter_end - m_new)
        C = decay[..., None, None] * C + jnp.einsum('bhl,bhld,bhle->bhde', s, k_i, v_i)
        n = decay[..., None] * n + jnp.einsum('bhl,bhld->bhd', s, k_i)
        return (C, n, m_new), h

    init = (jnp.zeros((b_, h_, d, d), q.dtype), jnp.zeros((b_, h_, d), q.dtype), jnp.zeros((b_, h_), q.dtype))
    _, h = lax.scan(step, init, (q, k, v, bcum, D, D_max, qk, w_end))
    return from_chunks(h)


def token_mixer(h, w_in, w_out, gdn_conv, gdn_a_log, gdn_dt_bias, gdn_norm,
                mlstm_conv, mlstm_i_bias, mlstm_f_bias, mlstm_norm):
    b_, t_, _ = h.shape
    dt = h.dtype
    f32 = jnp.float32
    qkv_a, a_a, b_a, z_a, qkv_b, qk_c, v_c, i_c, f_c, o_c = jnp.split(h @ w_in, IN_SPLITS, axis=-1)
    qa, ka, va = jnp.split(causal_conv_silu(qkv_a, gdn_conv), 3, axis=-1)
    qa = l2norm(split_heads(qa).astype(f32))
    ka = l2norm(split_heads(ka).astype(f32))
    va = split_heads(va).astype(f32)
    g = -jnp.exp(gdn_a_log.astype(f32)) * jax.nn.softplus((a_a + gdn_dt_bias).astype(f32))
    beta = jax.nn.sigmoid(b_a.astype(f32))
    oa = gated_delta_rule(qa, ka, va, g.transpose(0, 2, 1), beta.transpose(0, 2, 1)).transpose(0, 2, 1, 3)
    oa = oa * lax.rsqrt(jnp.mean(oa * oa, -1, keepdims=True) + NORM_EPS) * gdn_norm
    oa = (oa.reshape(b_, t_, W_GDN) * jax.nn.silu(z_a.astype(f32))).astype(dt)
    qb, kb, vb = jnp.split(qkv_b, 3, axis=-1)
    ob = moba_attention(split_heads(qb), split_heads(kb), split_heads(vb))
    ob = ob.transpose(0, 2, 1, 3).reshape(b_, t_, W_MOBA).astype(dt)
    qc, kc = jnp.split(causal_conv_silu(qk_c, mlstm_conv), 2, axis=-1)
    i_pre = (i_c + mlstm_i_bias).astype(f32).transpose(0, 2, 1)
    logf = jax.nn.log_sigmoid((f_c + mlstm_f_bias).astype(f32)).transpose(0, 2, 1)
    hc = mlstm_chunkwise(split_heads(qc).astype(f32), split_heads(kc).astype(f32),
                         split_heads(v_c).astype(f32), i_pre, logf).transpose(0, 2, 1, 3)
    hc = hc * jax.nn.sigmoid(o_c.astype(f32)).reshape(b_, t_, H_MLSTM, HEAD_DIM)
    mu = jnp.mean(hc, -1, keepdims=True)
    var = jnp.mean(jnp.square(hc - mu), -1, keepdims=True)
    hc = (hc - mu) * lax.rsqrt(var + NORM_EPS) * mlstm_norm.reshape(H_MLSTM, HEAD_DIM)
    hc = hc.reshape(b_, t_, W_MLSTM).astype(dt)
    return jnp.concatenate([oa, ob, hc], axis=-1) @ w_out


def setup_inputs(seed: int = 0) -> dict:
    key = jax.random.key(seed)
    ks = jax.random.split(key, 18)
    nrm = lambda k, shape, s: jax.random.normal(k, shape, jnp.float32) * s
    dt_init = jnp.exp(jax.random.uniform(ks[10], (DEPTH, H_GDN), jnp.float32,
                                         float(np.log(1e-3)), float(np.log(1e-1))))
    return {
        "x": nrm(ks[0], (BATCH, SEQ, D_MODEL), 1.0),
        "c": nrm(ks[1], (BATCH, D_MODEL), 1.0),
        "ada_w": nrm(ks[2], (DEPTH, D_MODEL, N_SUB * 3 * D_MODEL), ADA_INIT * D_MODEL ** -0.5),
        "ada_b": nrm(ks[3], (DEPTH, N_SUB * 3 * D_MODEL), 0.01),
        "ffn_w13": nrm(ks[4], (DEPTH, 2, D_MODEL, 2 * D_FF), D_MODEL ** -0.5),
        "ffn_w2": nrm(ks[5], (DEPTH, 2, D_FF, D_MODEL), BETA_INIT * D_FF ** -0.5),
        "w_in": nrm(ks[6], (DEPTH, D_MODEL, D_IN), D_MODEL ** -0.5),
        "w_out": nrm(ks[7], (DEPTH, MIX_WIDTH, D_MODEL), BETA_INIT * MIX_WIDTH ** -0.5),
        "gdn_conv": nrm(ks[8], (DEPTH, CONV_WIDTH, 3 * W_GDN), CONV_WIDTH ** -0.5),
        "gdn_a_log": jnp.log(jax.random.uniform(ks[9], (DEPTH, H_GDN), jnp.float32, 1.0, 16.0)),
        "gdn_dt_bias": dt_init + jnp.log(-jnp.expm1(-dt_init)),
        "gdn_norm": 1.0 + nrm(ks[11], (DEPTH, HEAD_DIM), 0.02),
        "mlstm_conv": nrm(ks[12], (DEPTH, CONV_WIDTH, 2 * W_MLSTM), CONV_WIDTH ** -0.5),
        "mlstm_i_bias": nrm(ks[13], (DEPTH, H_MLSTM), 0.1),
        "mlstm_f_bias": jnp.linspace(3.0, 6.0, H_MLSTM, dtype=jnp.float32)[None, :] + nrm(ks[14], (DEPTH, H_MLSTM), 0.1),
        "mlstm_norm": 1.0 + nrm(ks[15], (DEPTH, W_MLSTM), 0.02),
        "ln_g": 1.0 + nrm(ks[16], (DEPTH, N_SUB, D_MODEL), 0.02),
        "ln_b": nrm(ks[17], (DEPTH, N_SUB, D_MODEL), 0.02),
    }


def reference(x, c, ada_w, ada_b, ffn_w13, ffn_w2, w_in, w_out, gdn_conv, gdn_a_log, gdn_dt_bias,
              gdn_norm, mlstm_conv, mlstm_i_bias, mlstm_f_bias, mlstm_norm, ln_g, ln_b):
    b_ = x.shape[0]
    for l in range(DEPTH):
        mod = (jax.nn.silu(c) @ ada_w[l] + ada_b[l]).reshape(b_, N_SUB, 3, D_MODEL)
        y = swiglu(modulate(x, mod[:, 0, 0], mod[:, 0, 1]), ffn_w13[l, 0], ffn_w2[l, 0])
        x = layer_norm(ALPHA * x + 0.5 * (1 + mod[:, 0, 2])[:, None, :] * y, ln_g[l, 0], ln_b[l, 0])
        y = token_mixer(modulate(x, mod[:, 1, 0], mod[:, 1, 1]), w_in[l], w_out[l], gdn_conv[l], gdn_a_log[l],
                        gdn_dt_bias[l], gdn_norm[l], mlstm_conv[l], mlstm_i_bias[l], mlstm_f_bias[l], mlstm_norm[l])
        x = layer_norm(ALPHA * x + (1 + mod[:, 1, 2])[:, None, :] * y, ln_g[l, 1], ln_b[l, 1])
        y = swiglu(modulate(x, mod[:, 2, 0], mod[:, 2, 1]), ffn_w13[l, 1], ffn_w2[l, 1])
        x = layer_norm(ALPHA * x + 0.5 * (1 + mod[:, 2, 2])[:, None, :] * y, ln_g[l, 2], ln_b[l, 2])
    return x
```

```python
import contextlib
import numpy as np
import concourse.bass as bass
import concourse.mybir as mybir
from concourse.bass_utils import run_bass_kernel_spmd

F32 = mybir.dt.float32
BF16 = mybir.dt.bfloat16
AF = mybir.ActivationFunctionType
ALU = mybir.AluOpType
AX = mybir.AxisListType


class Res:
    __slots__ = ("name", "lw", "rd", "excl")

    def __init__(self, name):
        self.name = name
        self.lw = None
        self.rd = {}
        self.excl = False


class Tile:
    def __init__(self, t, name):
        self.t = t
        self.res = Res(name)

    def __getitem__(self, k):
        return self.t[k]


class Prog:
    CENG = ("pe", "act", "dve", "pool")

    def __init__(self, nc, ndma=24):
        self.nc = nc
        self.es = contextlib.ExitStack()
        self.streams = {e: [] for e in self.CENG + ("sp",)}
        self.cnt = {}
        self.sems = {}
        self.waited = {e: {} for e in self.CENG + ("sp",)}
        for e in self.CENG:
            self._mksem(e)
        self.ndma = ndma
        for j in range(ndma):
            self._mksem(("dma", j))
        self.dma_rr = 0
        self.nuid = 0

    def _mksem(self, key):
        nm = key if isinstance(key, str) else "%s%d" % key
        self.sems[key] = self.es.enter_context(self.nc.semaphore("s_" + nm))
        self.cnt[key] = 0

    def sb(self, name, shape, dtype=F32):
        t = self.es.enter_context(self.nc.sbuf_tensor("sb_" + name, list(shape), dtype))
        return Tile(t, name)

    def ps(self, name, shape, dtype=F32):
        t = self.es.enter_context(self.nc.psum_tensor("ps_" + name, list(shape), dtype))
        tl = Tile(t, name)
        tl.res.excl = True
        return tl

    def dram(self, name, shape, dtype=F32, kind="Internal"):
        t = self.nc.dram_tensor(name, list(shape), dtype, kind=kind)
        return Tile(t.ap(), name)

    def _deps(self, reads, writes):
        deps = []
        for r in reads:
            r = r.res if hasattr(r, 'res') else r
            if r.lw is not None:
                deps.append(r.lw)
            if r.excl:
                deps.extend(r.rd.items())
        for w in writes:
            w = w.res if hasattr(w, 'res') else w
            if w.lw is not None:
                deps.append(w.lw)
            deps.extend(w.rd.items())
        return deps

    def _commit(self, reads, writes, key, val):
        for r in reads:
            r = r.res if hasattr(r, 'res') else r
            r.rd[key] = val
        for w in writes:
            w = w.res if hasattr(w, 'res') else w
            w.lw = (key, val)
            w.rd = {}

    def _waits(self, eng, deps, skip_self=False):
        best = {}
        for k, v in deps:
            if skip_self and k == eng:
                continue
            if v > best.get(k, 0):
                best[k] = v
        out = []
        wd = self.waited[eng]
        for k, v in best.items():
            if wd.get(k, 0) >= v:
                continue
            wd[k] = v
            out.append((k, v))
        return out

    def op(self, eng, fn, reads=(), writes=()):
        deps = self._deps(reads, writes)
        waits = self._waits(eng, deps, skip_self=(eng == "pe"))
        self.cnt[eng] += 1
        val = self.cnt[eng]
        self.streams[eng].append((waits, fn, eng, 1))
        self._commit(reads, writes, eng, val)

    def dma(self, out, in_, reads=(), writes=(), q="sp", **kw):
        key = ("dma", self.dma_rr)
        self.dma_rr = (self.dma_rr + 1) % self.ndma
        deps = self._deps(reads, writes)
        deps.append((key, self.cnt[key]))
        waits = self._waits(q, deps)
        self.cnt[key] += 16
        val = self.cnt[key]

        def fn(e, out=out, in_=in_, kw=kw):
            return e.dma_start(out=out, in_=in_, **kw)

        self.streams[q].append((waits, fn, key, 16))
        self._commit(reads, writes, key, val)

    def emit(self):
        nc = self.nc
        fin = [(k, v) for k, v in self.cnt.items() if v > 0]
        with nc.Block() as block:
            def run(eng_name):
                def body(e):
                    for waits, fn, key, inc in self.streams[eng_name]:
                        for k, v in waits:
                            e.wait_ge(self.sems[k], v)
                        fn(e).then_inc(self.sems[key], inc)
                    if eng_name == "sp":
                        for k, v in fin:
                            e.wait_ge(self.sems[k], v)
                return body

            block.tensor(run("pe"))
            block.scalar(run("act"))
            block.vector(run("dve"))
            block.gpsimd(run("pool"))
            block.sync(run("sp"))
        self.es.close()

    def mm(self, out, lhsT, rhs, start=True, stop=True, reads=(), writes=()):
        self.op("pe", lambda e: e.matmul(out, lhsT, rhs, start=start, stop=stop),
                reads, writes)

    def tr(self, out, in_, ident, reads=(), writes=()):
        self.op("pe", lambda e: e.transpose(out, in_, ident), reads, writes)

    def act(self, out, in_, func, bias=None, scale=None, accum_out=None,
            reads=(), writes=()):
        kw = {}
        if bias is not None:
            kw["bias"] = bias
        if scale is not None:
            kw["scale"] = scale
        if accum_out is not None:
            kw["accum_out"] = accum_out
        self.op("act", lambda e: e.activation(out, in_, func, **kw), reads, writes)


D = 1024
T = 8192
NB = 4
DFF = 2816
NTC = 4096
ALPHA = 4.0 ** 0.25
LN_EPS = 1e-5
NEPS = 1e-6
NJ = DFF // 128
TT = 256


def _ext(nc, name, shape, kind="ExternalInput", dtype=F32):
    return nc.dram_tensor(name, list(shape), dtype, kind=kind).ap()


def build_mod_prog():
    nc = bass.Bass("TRN2", target_bir_lowering=False)
    cT = _ext(nc, "cT", [128, 8, NB])
    aw = _ext(nc, "aw", [2, D, 1152])
    ab = _ext(nc, "ab", [2, 1, 1152])
    out = _ext(nc, "mod", [2, NB, 1152], kind="ExternalOutput")
    P = Prog(nc)
    ct = P.sb("ct", [128, 8, NB])
    sc = P.sb("sc", [128, 8, NB])
    wt = [P.sb("wt%d" % i, [128, 8, 1152]) for i in range(2)]
    bt = P.sb("bt", [NB, 2, 1152])
    ot = P.sb("ot", [NB, 2, 1152])
    ps = [P.ps("ps%d" % i, [128, 512]) for i in range(3)]
    P.dma(ct[:], cT, writes=[ct])
    P.act(sc[:], ct[:], AF.Silu, reads=[ct], writes=[sc])
    for l in range(2):
        P.dma(wt[l][:], aw[l].rearrange("(c p) n -> p c n", p=128), writes=[wt[l]])
        P.dma(bt[:, l, :], ab[l].partition_broadcast(NB), writes=[bt])
    for l in range(2):
        for g in range(3):
            for kc in range(8):
                P.mm(ps[g][0:NB, 0:384], sc[:, kc, :], wt[l][:, kc, g * 384:(g + 1) * 384],
                     start=(kc == 0), stop=(kc == 7), reads=[sc, wt[l]], writes=[ps[g]])
            P.op("dve", lambda e, l=l, g=g: e.tensor_tensor(
                ot[:, l, g * 384:(g + 1) * 384], ps[g][0:NB, 0:384], bt[:, l, g * 384:(g + 1) * 384], ALU.add),
                [ps[g], bt], [ot])
    P.dma(out.rearrange("l b n -> b l n"), ot[:], reads=[ot])
    P.emit()
    return nc


def build_dense_prog(kinds):
    nc = bass.Bass("TRN2", target_bir_lowering=False)
    P = Prog(nc)
    nph = len(kinds)
    xin = Tile(_ext(nc, "xin", [NTC, D]), "xin")
    xout = Tile(_ext(nc, "xout", [NTC, D], kind="ExternalOutput"), "xout")
    uin = Tile(_ext(nc, "uin", [NTC, D]), "uin") if "lin" in kinds else None
    identd = _ext(nc, "ident", [128, 128])
    ph_in = []
    for i, k in enumerate(kinds):
        d = {"kind": k}
        d["gate"] = _ext(nc, "gate%d" % i, [1, D])
        d["lng"] = _ext(nc, "lng%d" % i, [1, D])
        d["lnb"] = _ext(nc, "lnb%d" % i, [1, D])
        if k == "ffn":
            d["modc"] = _ext(nc, "modc%d" % i, [128, 16])
            d["w13"] = _ext(nc, "w13_%d" % i, [D, 2 * DFF])
            d["w2"] = _ext(nc, "w2_%d" % i, [DFF, D])
        else:
            d["wout"] = _ext(nc, "wout%d" % i, [D, D])
        ph_in.append(d)
    scratch = [P.dram("xs%d" % i, [NTC, D]) for i in range(nph - 1)]

    w13b = P.sb("w13b", [128, 8, 2 * DFF], BF16)
    w2b = P.sb("w2b", [128, NJ, D], BF16)
    stage = P.sb("stage", [128, 2048])
    xt = [P.sb("xt%d" % i, [128, 2, D]) for i in range(2)]
    ut = [P.sb("ut0", [128, 2, D])] * 2 if uin is not None else None
    uT = [P.sb("uT%d" % i, [128, 8, TT], BF16) for i in range(2)]
    gT = P.sb("gT", [128, NJ, TT], BF16)
    gres = [Res("g%d" % j) for j in range(NJ)]
    sa = [P.sb("sa%d" % i, [128, TT]) for i in range(2)]
    t1 = [P.sb("t1_%d" % i, [128, D]) for i in range(2)]
    gf = P.sb("gf", [128, D]); lng = P.sb("lng", [128, D]); lnb = P.sb("lnb", [128, D])
    ident = P.sb("ident", [128, 128])
    modc = P.sb("modc", [128, 16])
    st = [P.sb("st%d" % i, [128, 2, 6]) for i in range(2)]
    mv = [P.sb("mv%d" % i, [128, 2]) for i in range(2)]
    epsb = P.sb("epsb", [128, 1])
    pT = P.ps("pT", [128, 512])
    phb = [P.ps("ph%d" % i, [128, 512]) for i in range(4)]
    pyb = [P.ps("py%d" % i, [128, 512]) for i in range(3)]

    P.dma(ident[:], identd, writes=[ident])
    P.op("pool", lambda e: e.memset(epsb[:], LN_EPS), [], [epsb])
    cast_rr = [0]

    def cast(out_ap, in_ap, reads, writes):
        eng = ("dve", "pool", "act")[cast_rr[0] % 3]
        cast_rr[0] += 1
        if eng == "act":
            P.op("act", lambda e: e.copy(out_ap, in_ap), reads, writes)
        else:
            P.op(eng, lambda e: e.tensor_copy(out_ap, in_ap), reads, writes)

    ygrp = [0]
    for pi, ph in enumerate(ph_in):
        kind = ph["kind"]
        src = xin if pi == 0 else scratch[pi - 1]
        dst = xout if pi == nph - 1 else scratch[pi]
        gmul = 0.5 if kind == "ffn" else 1.0
        P.dma(gf[:], ph["gate"].partition_broadcast(128), writes=[gf])
        P.op("pool", lambda e, gmul=gmul: e.tensor_scalar(gf[:], gf[:], 1.0, gmul, ALU.add, ALU.mult), [gf], [gf])
        P.dma(lng[:], ph["lng"].partition_broadcast(128), writes=[lng])
        P.dma(lnb[:], ph["lnb"].partition_broadcast(128), writes=[lnb])
        if kind == "ffn":
            P.dma(modc[:], ph["modc"], writes=[modc])
            P.op("pool", lambda e: e.tensor_scalar_add(modc[:, 0:8], modc[:, 0:8], 1.0), [modc], [modc])
            for kc in range(8):
                for h in range(4):
                    P.dma(stage[:, 0:1408], ph["w13"][kc * 128:(kc + 1) * 128, h * 1408:(h + 1) * 1408], writes=[stage])
                    cast(w13b[:, kc, h * 1408:(h + 1) * 1408], stage[:, 0:1408], [stage], [w13b])
            for j in range(0, NJ, 2):
                P.dma(stage[:, 0:2 * D].rearrange("p (j d) -> p j d", j=2),
                      ph["w2"][j * 128:(j + 2) * 128, :].rearrange("(j p) d -> p j d", p=128), writes=[stage])
                cast(w2b[:, j:j + 2, :], stage[:, 0:2 * D].rearrange("p (j d) -> p j d", j=2), [stage], [w2b])
        else:
            for kc in range(0, 8, 2):
                P.dma(stage[:, 0:2 * D].rearrange("p (j d) -> p j d", j=2),
                      ph["wout"][kc * 128:(kc + 2) * 128, :].rearrange("(j p) d -> p j d", p=128), writes=[stage])
                cast(w2b[:, kc:kc + 2, :], stage[:, 0:2 * D].rearrange("p (j d) -> p j d", j=2), [stage], [w2b])

        ntile = NTC // TT

        def load(it):
            sl = slice(it * TT, (it + 1) * TT)
            P.dma(xt[it % 2][:], src[sl, :].rearrange("(s p) d -> p s d", p=128), reads=[src], writes=[xt[it % 2]])
            if kind == "lin":
                P.dma(ut[it % 2][:], uin[sl, :].rearrange("(s p) d -> p s d", p=128), reads=[uin], writes=[ut[it % 2]])

        def transposes(it, cp):
            tsrc = ut[it % 2] if kind == "lin" else xt[it % 2]
            u = uT[it % 2]
            for c in (2 * cp, 2 * cp + 1):
                for s in range(2):
                    o0 = (c % 2) * 256 + s * 128
                    P.tr(pT[:, o0:o0 + 128], tsrc[:, s, c * 128:(c + 1) * 128], ident[:], [tsrc, ident], [pT])
            for c in (2 * cp, 2 * cp + 1):
                o0 = (c % 2) * 256
                if kind == "ffn":
                    P.op("dve", lambda e, c=c, o0=o0, u=u: e.tensor_scalar(
                        u[:, c, :], pT[:, o0:o0 + 256], modc[:, c:c + 1], modc[:, 8 + c:9 + c], ALU.mult, ALU.add),
                        [pT, modc], [u])
                else:
                    P.op("dve", lambda e, c=c, o0=o0, u=u: e.tensor_copy(u[:, c, :], pT[:, o0:o0 + 256]), [pT], [u])

        def hstage(it):
            u = uT[it % 2]
            for j in range(NJ):
                pa = phb[(2 * j) % 4]; pb = phb[(2 * j + 1) % 4]
                for kc in range(8):
                    P.mm(pa[:, 0:TT], w13b[:, kc, j * 128:(j + 1) * 128], u[:, kc, :],
                         start=(kc == 0), stop=(kc == 7), reads=[w13b, u], writes=[pa])
                for kc in range(8):
                    P.mm(pb[:, 0:TT], w13b[:, kc, DFF + j * 128:DFF + (j + 1) * 128], u[:, kc, :],
                         start=(kc == 0), stop=(kc == 7), reads=[w13b, u], writes=[pb])
                s_ = sa[j % 2]
                P.act(s_[:], pa[:, 0:TT], AF.Silu, reads=[pa], writes=[s_])
                P.op("dve", lambda e, j=j, s_=s_, pb=pb: e.tensor_tensor(gT[:, j, :], s_[:], pb[:, 0:TT], ALU.mult),
                     [s_, pb], [gres[j]])

        def ygroup(it, s, half):
            py = pyb[ygrp[0] % 3]
            ygrp[0] += 1
            if kind == "ffn":
                for j in range(NJ):
                    P.mm(py[:, :], gT[:, j, s * 128:(s + 1) * 128], w2b[:, j, half * 512:(half + 1) * 512],
                         start=(j == 0), stop=(j == NJ - 1), reads=[gres[j], w2b], writes=[py])
            else:
                u = uT[it % 2]
                for kc in range(8):
                    P.mm(py[:, :], u[:, kc, s * 128:(s + 1) * 128], w2b[:, kc, half * 512:(half + 1) * 512],
                         start=(kc == 0), stop=(kc == 7), reads=[u, w2b], writes=[py])
            t = t1[s]
            P.op("dve", lambda e, t=t, py=py, half=half: e.tensor_tensor(
                t[:, half * 512:(half + 1) * 512], py[:, :], gf[:, half * 512:(half + 1) * 512], ALU.mult),
                [py, gf], [t])

        def epilogue(it, s):
            t = t1[s]; x_ = xt[it % 2]; st_ = st[s]; mv_ = mv[s]
            P.op("dve", lambda e: e.scalar_tensor_tensor(t[:], x_[:, s, :], ALPHA, t[:], ALU.mult, ALU.add), [x_, t], [t])
            for hh in range(2):
                P.op("dve", lambda e, hh=hh: e.bn_stats(st_[:, hh, :], t[:, hh * 512:(hh + 1) * 512]), [t], [st_])
            P.op("dve", lambda e: e.bn_aggr(mv_[:], st_[:].rearrange("p a b -> p (a b)")), [st_], [mv_])
            P.act(mv_[:, 1:2], mv_[:, 1:2], AF.Ln, bias=epsb[:, 0:1], reads=[mv_, epsb], writes=[mv_])
            P.act(mv_[:, 1:2], mv_[:, 1:2], AF.Exp, scale=-0.5, reads=[mv_], writes=[mv_])
            P.op("dve", lambda e: e.tensor_scalar(t[:], t[:], mv_[:, 0:1], mv_[:, 1:2], ALU.subtract, ALU.mult), [t, mv_], [t])
            P.op("pool", lambda e: e.tensor_tensor(t[:], t[:], lng[:], ALU.mult), [t, lng], [t])
            P.op("pool", lambda e: e.tensor_tensor(t[:], t[:], lnb[:], ALU.add), [t, lnb], [t])
            r0 = it * TT + s * 128
            P.dma(dst[r0:r0 + 128, :], t[:], reads=[t], writes=[dst])

        load(0)
        for cp in range(4):
            transposes(0, cp)
        for it in range(ntile):
            if it + 1 < ntile:
                load(it + 1)
            if kind == "ffn":
                hstage(it)
            g = 0
            for s in range(2):
                for half in range(2):
                    if it + 1 < ntile:
                        transposes(it + 1, g)
                    ygroup(it, s, half)
                    g += 1
                epilogue(it, s)
    P.emit()
    return nc


_PROGS = {}


def _prog(key, builder):
    if key not in _PROGS:
        _PROGS[key] = builder()
    return _PROGS[key]


def _run(nc, in_maps):
    res = run_bass_kernel_spmd(nc, in_maps, core_ids=list(range(8)))
    return res.results


def run_mod(c, ada_w, ada_b):
    nc = _prog("mod", build_mod_prog)
    cT = np.ascontiguousarray(c.T.reshape(8, 128, NB).transpose(1, 0, 2))
    maps = []
    for k in range(8):
        sl = slice(k * 1152, (k + 1) * 1152)
        maps.append({"cT": cT, "aw": np.ascontiguousarray(ada_w[:, :, sl]),
                     "ab": np.ascontiguousarray(ada_b[:, None, sl])})
    r = _run(nc, maps)
    mod = np.concatenate([r[k]["mod"] for k in range(8)], axis=-1)
    return mod.reshape(2, NB, 3, 3, D)


def _cols(v):
    return np.ascontiguousarray(v.reshape(8, 128).T)


def run_dense(kinds, x, u, phase_params):
    nc = _prog(("dense",) + tuple(kinds), lambda: build_dense_prog(kinds))
    ident = np.eye(128, dtype=np.float32)
    maps = []
    for k in range(8):
        b = k // 2
        m = {"xin": x[k * NTC:(k + 1) * NTC], "ident": ident}
        if u is not None:
            m["uin"] = u[k * NTC:(k + 1) * NTC]
        for i, (kind, pp) in enumerate(zip(kinds, phase_params)):
            mod = pp["mod"]
            m["gate%d" % i] = np.ascontiguousarray(mod[b, 2][None, :])
            m["lng%d" % i] = np.ascontiguousarray(pp["lng"][None, :])
            m["lnb%d" % i] = np.ascontiguousarray(pp["lnb"][None, :])
            if kind == "ffn":
                m["modc%d" % i] = np.ascontiguousarray(np.concatenate([_cols(mod[b, 1]), _cols(mod[b, 0])], axis=1))
                m["w13_%d" % i] = pp["w13"]
                m["w2_%d" % i] = pp["w2"]
            else:
                m["wout%d" % i] = pp["wout"]
        maps.append(m)
    r = _run(nc, maps)
    return np.concatenate([r[k]["xout"] for k in range(8)], axis=0)


class PSlot:
    def __init__(self, bank, c0, name):
        self.t = bank.t
        self.c0 = c0
        self.res = bank.res

    def __getitem__(self, k):
        r, c = k
        a = self.c0 + (c.start or 0)
        b = self.c0 + c.stop
        return self.t[r, a:b]


NST = T // 512
STAGES = {'gates', 'tok', 'gdn', 'mlstm', 'moba'}
LEVEL = 99
L2MODE = 0
L2SET = ((0, 128), (1, 128), (3, 64), (4, 64))
BIGM = 30000.0


def build_mixer_prog(nst=NST):
    nc = bass.Bass("TRN2", target_bir_lowering=False)
    P = Prog(nc)
    x1 = Tile(_ext(nc, "x1", [T, D]), "x1")
    mix = Tile(_ext(nc, "mix", [T, 512], kind="ExternalOutput"), "mix")
    modcd = _ext(nc, "modc", [128, 16])
    wind = _ext(nc, "win", [D, 1932])
    convd = _ext(nc, "convw", [128, 8, 4])
    gpard = _ext(nc, "gpar", [3, 4])
    gnormd = _ext(nc, "gnorm", [1, 192])
    mnormd = _ext(nc, "mnorm", [1, 192])
    cstd = _ext(nc, "cst", [128, 6, 128])
    seld = _ext(nc, "sel", [3, 3, 128])
    rstd = _ext(nc, "rstm", [3, 512])
    ohd = _ext(nc, "onehot", [32, T])
    wmd = _ext(nc, "wmask", [1, 64])

    winb = P.sb("winb", [128, 8, 1932], BF16)
    xt = [P.sb("xtm", [128, 2, D])]
    xmT = P.sb("xmT", [128, 8, 512], BF16)
    modc = P.sb("modc", [128, 16])
    convw = P.sb("convw", [128, 8, 4])
    gpar = P.sb("gpar", [3, 4])
    gnorm = P.sb("gnorm", [128, 192]); mnorm = P.sb("mnorm", [128, 192])
    cst = P.sb("cst", [128, 6, 128])
    ident = cst[:, 0, :]; mb_incl = cst[:, 1, :]; mb_strT = cst[:, 2, :]; maskU = cst[:, 3, :]; bdones = cst[:, 4, :]
    maskUb = P.sb("maskUb", [128, 128], BF16)
    bdb = P.sb("bdb", [128, 128], BF16); sqb = P.sb("sqb", [128, 512], BF16)
    sel = P.sb("sel", [3, 3, 128])
    rstm = P.sb("rstm", [3, 512])
    wm = P.sb("wm", [128, 64])
    ones1 = P.sb("ones1", [128, 1]); epsn = P.sb("epsn", [128, 1]); ln8 = P.sb("ln8", [128, 1])
    onesrow = P.sb("onesrow", [1, 128])
    cbody = P.sb("cbody", [128, 515]); halos = P.sb("halos", [128, 8, 3])
    cacc = P.sb("cacc", [128, 512])
    fm = [P.sb("fm%d" % i, [128, 512]) for i in range(8)]
    sq = P.sb("sq", [128, 512]); rn = P.sb("rn", [128, 512])
    tma = P.sb("tma", [128, 4, 512]); tmb = P.sb("tmb", [128, 4, 192])
    stage = Tile(tma.t[:].rearrange("p a c -> p (a c)"), "stg"); stage.res = tma.res
    ga = P.sb("ga", [3, 512]); gb = P.sb("gb", [3, 512]); gi = P.sb("gi", [3, 512]); gf_ = P.sb("gf_", [3, 512])
    gt1 = P.sb("gt1", [3, 512]); gt2 = gt1
    nA = P.sb("nA", [3, 1])
    Grow = P.sb("Grow", [3, 512]); betar = P.sb("betar", [3, 512]); bxg = P.sb("bxg", [3, 512]); edr = P.sb("edr", [3, 512])
    glr = P.sb("glr", [3, 4])
    brow = P.sb("brow", [3, 512]); ws8 = P.sb("ws8", [3, 512]); thr = P.sb("thr", [3, 512]); apr = gi
    mloc = P.sb("mloc", [3, 4]); mc = P.sb("mc", [3, 4]); air = P.sb("air", [3, 4]); mprev = P.sb("mprev", [3, 1])
    tmg = P.sb("tmg", [128, 4, 24])
    glc = P.sb("glc", [128, 3, 4]); aic = P.sb("aic", [128, 3, 4])
    KA = [P.sb("KA%d" % i, [128, 128]) for i in range(3)]
    KB = [P.sb("KB%d" % i, [128, 128]) for i in range(3)]
    vb = P.sb("vb", [128, 192])
    vxm = P.sb("vxm", [128, 3, 65])
    Am = [[P.sb("A%d_%d" % (h, i), [128, 128]) for i in range(2)] for h in range(3)]
    Bm = [[P.sb("B%d_%d" % (h, i), [128, 128]) for i in range(2)] for h in range(3)]
    Rm = [P.sb("R%d" % h, [128, 128]) for h in range(3)]
    dmt = [P.sb("dmt%d" % h, [128, 128]) for h in range(3)]
    dma_ = [P.sb("dmA%d" % h, [128, 128]) for h in range(3)]
    attT = [P.sb("attT%d" % h, [128, 128]) for h in range(3)]
    usb = [P.sb("usb%d" % h, [128, 64]) for h in range(3)]
    wTs = [P.sb("wTs%d" % h, [128, 128]) for h in range(3)]
    vnew = [P.sb("vnew%d" % h, [128, 64]) for h in range(3)]
    Sg = [P.sb("Sg%d" % h, [128, 64]) for h in range(3)]
    og = [P.sb("og%d" % h, [128, 64]) for h in range(3)]
    Cm = [P.sb("Cm%d" % h, [128, 65]) for h in range(3)]
    PTm = [P.sb("PTm%d" % h, [128, 128]) for h in range(3)]
    nda = [P.sb("nda%d" % h, [128, 65]) for h in range(3)]
    hm = P.sb("hm", [128, 192])
    szz = P.sb("szz", [128, 192]); sgo = szz
    st6 = P.sb("st6", [128, 6]); mv2 = P.sb("mv2", [128, 2]); sm1 = P.sb("sm1", [128, 4])
    kext = [P.sb("kext%d" % h, [96, T], BF16) for h in range(2)]
    qext = [P.sb("qext%d" % h, [96, 512], BF16) for h in range(2)]
    qf = [P.sb("qf%d" % h, [64, 512]) for h in range(2)]
    vxb = [P.sb("vxb%d" % h, [128, 64, 65], BF16) for h in range(2)]
    kmean = [P.sb("kmean%d" % h, [64, 32]) for h in range(2)]
    kmx = [P.sb("kmx%d" % h, [1, 2]) for h in range(2)]
    kmxc = P.sb("kmxc", [128, 1]); negm = P.sb("negm", [128, 2])
    gm = P.sb("gm", [128, 32]); top8 = P.sb("top8", [128, 8]); sbt = P.sb("sbt", [128, 96])
    PT = [P.sb("PT%d" % i, [128, 512], BF16) for i in range(2)]
    mo = [P.sb("mo0", [128, 4, 512])] * 2
    bk = [P.ps("bk%d" % i, [128, 512]) for i in range(8)]
    pIn = [bk[0], bk[1]]
    gmat = [PSlot(bk[2], 128 * i, "gmat%d" % i) for i in range(4)]
    chs = [PSlot(bk[3], 0, "c0a"), PSlot(bk[3], 128, "c0b"), PSlot(bk[3], 256, "c1a"), PSlot(bk[3], 384, "c1b"),
           PSlot(bk[5], 0, "c2a"), PSlot(bk[5], 128, "c2b")]
    msc = [PSlot(bk[5], 256, "m0"), PSlot(bk[5], 384, "m1")]
    ptr = [PSlot(bk[4], 128 * i, "ptr%d" % i) for i in range(4)]
    pST = [bk[6]]
    pacc = [PSlot(bk[7], 65 * i, "pacc%d" % i) for i in range(4)]
    sml = [PSlot(bk[7], 260, "s0"), PSlot(bk[7], 332, "s1"), PSlot(bk[7], 404, "s2"), msc[0], msc[1]]
    smlrr = [0]

    def small():
        s_ = sml[smlrr[0] % len(sml)]
        smlrr[0] += 1
        return s_

    evrr = [0]

    def evac(out_ap, in_ap, reads, writes, eng=None):
        if eng is None:
            eng = ("dve", "act")[evrr[0] % 2]
            evrr[0] += 1
        if eng == "act":
            P.op("act", lambda e: e.copy(out_ap, in_ap), reads, writes)
        else:
            P.op(eng, lambda e: e.tensor_copy(out_ap, in_ap), reads, writes)

    V = lambda f, r, w: P.op("dve", f, r, w)
    G = lambda f, r, w: P.op("pool", f, r, w)
    AL = slice(None)

    for (t_, d_) in ((modc, modcd), (convw, convd), (gpar, gpard), (cst, cstd), (sel, seld), (rstm, rstd)):
        P.dma(t_[:], d_, writes=[t_])
    P.dma(gnorm[:], gnormd.partition_broadcast(128), writes=[gnorm])
    P.dma(mnorm[:], mnormd.partition_broadcast(128), writes=[mnorm])
    P.dma(wm[:], wmd.partition_broadcast(128), writes=[wm])
    G(lambda e: e.tensor_scalar_add(modc[:, 0:8], modc[:, 0:8], 1.0), [modc], [modc])
    G(lambda e: e.memset(ones1[:], 1.0), [], [ones1])
    G(lambda e: e.memset(epsn[:], NEPS), [], [epsn])
    G(lambda e: e.memset(ln8[:], float(np.log(0.125))), [], [ln8])
    G(lambda e: e.memset(onesrow[:], 1.0), [], [onesrow])
    G(lambda e: e.tensor_copy(maskUb[:], maskU), [cst], [maskUb])
    G(lambda e: e.tensor_copy(bdb[:], bdones), [cst], [bdb])
    P.act(nA[:], gpar[:, 0:1], AF.Exp, reads=[gpar], writes=[nA])
    G(lambda e: e.tensor_scalar_mul(nA[:], nA[:], -1.0), [nA], [nA])
    for h in range(3):
        G(lambda e, h=h: e.memset(Sg[h][:], 0.0), [], [Sg[h]])
        G(lambda e, h=h: e.memset(Cm[h][:], 0.0), [], [Cm[h]])
    G(lambda e: e.memset(mprev[:], 0.0), [], [mprev])
    for i in range(3):
        G(lambda e, i=i: e.memset(KB[i][:], 0.0), [], [KB[i]])
        G(lambda e, i=i: e.memset(KA[i][:], 0.0), [], [KA[i]])
    G(lambda e: e.memset(vxm[:], 1.0), [], [vxm])
    G(lambda e: e.memset(sbt[:], 0.0), [], [sbt])
    G(lambda e: e.memset(halos[:], 0.0), [], [halos])
    for h in range(2):
        G(lambda e, h=h: e.memset(vxb[h][:], 1.0), [], [vxb[h]])
        G(lambda e, h=h: e.memset(kmx[h][:], 0.0), [], [kmx[h]])
        for q4 in range(4):
            P.dma(stage[64:96, :], ohd[:, q4 * 2048:(q4 + 1) * 2048], writes=[stage])
            V(lambda e, h=h, q4=q4: e.tensor_copy(kext[h][64:96, q4 * 2048:(q4 + 1) * 2048], stage[64:96, :]), [stage], [kext[h]])
    for kc in range(8):
        P.dma(stage[:, 0:1932], wind[kc * 128:(kc + 1) * 128, :], writes=[stage])
        evac(winb[:, kc, :], stage[:, 0:1932], [stage], [winb])

    FMOFF = [0, 128, 256, 384, 512, 640, 704, 832]
    FMM = [128, 128, 128, 128, 128, 64, 128, 128]
    MOFF = 960; GOFF = 1216; TAOFF = 1228; TBOFF = 1740
    SLOT = [(0, 1, 0), (0, 1, 1), (3, 4, 0), (3, 4, 1), (6, 7, 0), (6, 7, 1)]
    VSL = [(2, 0), (2, 1), (5, 0)]
    KCH = {1: 0, 4: 1, 7: 2}

    def hs_rows(half):
        return slice(64 * half, 64 * half + 64)

    inrr = [0]

    def pin():
        b = pIn[inrr[0] % 2]
        inrr[0] += 1
        return b

    def load_x(st, hf):
        r0 = st * 512 + hf * 256
        P.dma(xt[0][:], x1[r0:r0 + 256, :].rearrange("(s p) d -> p s d", p=128), reads=[x1], writes=[xt[0]])

    def inproj(st):
        xs = xt[0]
        for hf in range(2):
            if hf == 1:
                load_x(st, 1)
            for c2 in range(4):
                b = pin()
                for cc in range(2):
                    c = 2 * c2 + cc
                    for s2 in range(2):
                        o0 = cc * 256 + s2 * 128
                        P.tr(b[:, o0:o0 + 128], xs[:, s2, c * 128:(c + 1) * 128], ident, [xs, cst], [b])
                for cc in range(2):
                    c = 2 * c2 + cc
                    V(lambda e, c=c, cc=cc, b=b, hf=hf: e.tensor_scalar(xmT[:, c, hf * 256:(hf + 1) * 256], b[:, cc * 256:(cc + 1) * 256],
                                                                       modc[:, c:c + 1], modc[:, 8 + c:9 + c], ALU.mult, ALU.add),
                      [b, modc], [xmT])
        if st + 1 < nst:
            load_x(st + 1, 0)
        if LEVEL < 1:
            return
        for i in range(8):
            b = pin(); M = FMM[i]
            for kc in range(8):
                P.mm(b[0:M, :], winb[:, kc, FMOFF[i]:FMOFF[i] + M], xmT[:, kc, :], start=(kc == 0), stop=(kc == 7),
                     reads=[winb, xmT], writes=[b])
            G(lambda e, i=i, M=M: e.tensor_copy(cbody[0:M, 0:3], halos[0:M, i, :]), [halos], [cbody])
            evac(cbody[0:M, 3:515], b[0:M, :], [b], [cbody])
            G(lambda e, i=i, M=M: e.tensor_copy(halos[0:M, i, :], cbody[0:M, 512:515]), [cbody], [halos])
            V(lambda e, i=i, M=M: e.tensor_scalar_mul(cacc[0:M, :], cbody[0:M, 0:512], convw[0:M, i, 0:1]), [cbody, convw], [cacc])
            for j in range(1, 4):
                V(lambda e, i=i, M=M, j=j: e.scalar_tensor_tensor(cacc[0:M, :], cbody[0:M, j:j + 512], convw[0:M, i, j:j + 1],
                                                                 cacc[0:M, :], ALU.mult, ALU.add), [cbody, convw, cacc], [cacc])
            P.act(sq[0:M, :], cacc[0:M, :], AF.Exp, scale=-1.0, reads=[cacc], writes=[sq])
            G(lambda e, M=M: e.tensor_scalar_add(sq[0:M, :], sq[0:M, :], 1.0), [sq], [sq])
            V(lambda e, M=M: e.reciprocal(sq[0:M, :], sq[0:M, :]), [sq], [sq])
            V(lambda e, i=i, M=M: e.tensor_tensor(fm[i][0:M, :], cacc[0:M, :], sq[0:M, :], ALU.mult), [cacc, sq], [fm[i]])
        if LEVEL < 2:
            return
        if L2MODE == 5:
            G(lambda e: e.memset(sqb[:], 1.0), [], [sqb])
            return
        if L2MODE == 6:
            P.act(sqb[:], cst[:, 0, :], AF.Copy, reads=[cst], writes=[sqb]) if False else P.op("act", lambda e: e.copy(sqb[:, 0:128], cst[:, 0, :]), [cst], [sqb])
            return
        if L2MODE == 9:
            P.act(sqb[:, :], fm[0][:, :], AF.Exp, reads=[fm[0]], writes=[sqb])
            return
        if L2MODE == 10:
            P.act(sqb[:, :], fm[0][:, :], AF.Square, reads=[fm[0]], writes=[sqb])
            P.op("act", lambda e: e.copy(sqb[:, :], fm[0][:, :]), [fm[0]], [sqb])
            return
        if L2MODE == 7:
            P.op("act", lambda e: e.copy(sqb[:, :], fm[7][:, :]), [fm[7]], [sqb])
            return
        if L2MODE == 8:
            P.op("act", lambda e: e.copy(sqb[:, :], fm[0][:, :]), [fm[0]], [sqb])
            return
        if L2MODE == 3:
            G(lambda e: e.memset(sqb[:], 1.0), [], [sqb])
        for i, M in L2SET:
            if L2MODE != 3:
                P.act(sqb[0:M, :], fm[i][0:M, :], AF.Square, reads=[fm[i]], writes=[sqb])
            if L2MODE == 2:
                continue
            b = pin()
            P.mm(b[0:M, :], bdb[0:M, 0:M], sqb[0:M, :], reads=[bdb, sqb], writes=[b])
            if L2MODE == 4:
                continue
            if L2MODE == 3:
                V(lambda e, M=M, b=b: e.tensor_copy(rn[0:M, :], b[0:M, :]), [b], [rn])
                continue
            P.act(rn[0:M, :], b[0:M, :], AF.Ln, bias=epsn[0:M, 0:1], reads=[b, epsn], writes=[rn])
            P.act(rn[0:M, :], rn[0:M, :], AF.Exp, scale=-0.5, reads=[rn], writes=[rn])
            qs = 0.125 if i in (0, 3) else 1.0
            V(lambda e, i=i, M=M, qs=qs: e.scalar_tensor_tensor(fm[i][0:M, :], fm[i][0:M, :], qs, rn[0:M, :], ALU.mult, ALU.mult), [fm[i], rn], [fm[i]])
        if LEVEL < 3:
            return
        for h in range(2):
            b = pin()
            for kc in range(8):
                P.mm(b[0:64, :], winb[:, kc, MOFF + 128 * h:MOFF + 128 * h + 64], xmT[:, kc, :], start=(kc == 0), stop=(kc == 7),
                     reads=[winb, xmT], writes=[b])
            evac(qf[h][:, :], b[0:64, :], [b], [qf[h]], eng="act")
            V(lambda e, h=h, b=b: e.tensor_scalar_mul(qext[h][0:64, :], b[0:64, :], 0.125), [b], [qext[h]])
            b = pin()
            for kc in range(8):
                P.mm(b[0:64, :], winb[:, kc, MOFF + 128 * h + 64:MOFF + 128 * h + 128], xmT[:, kc, :], start=(kc == 0), stop=(kc == 7),
                     reads=[winb, xmT], writes=[b])
            V(lambda e, h=h, b=b: e.tensor_copy(kext[h][0:64, st * 512:(st + 1) * 512], b[0:64, :]), [b], [kext[h]])
            V(lambda e, h=h, b=b: e.tensor_reduce(kmean[h][:, 2 * st:2 * st + 2], b[0:64, :].rearrange("p (a c) -> p a c", a=2),
                                                 AX.X, ALU.add), [b], [kmean[h]])
            G(lambda e, h=h: e.tensor_scalar_mul(kmean[h][:, 2 * st:2 * st + 2], kmean[h][:, 2 * st:2 * st + 2], 1.0 / 256), [kmean[h]], [kmean[h]])
            P.act(sq[0:64, :], b[0:64, :], AF.Square, reads=[b], writes=[sq])
            s_ = pin()
            P.mm(s_[0:1, :], ones1[0:64, 0:1], sq[0:64, :], reads=[ones1, sq], writes=[s_])
            V(lambda e, h=h, s_=s_: e.tensor_reduce(kmx[h][:, 1:2], s_[0:1, :], AX.X, ALU.max), [s_], [kmx[h]])
            V(lambda e, h=h: e.tensor_tensor(kmx[h][:, 0:1], kmx[h][:, 0:1], kmx[h][:, 1:2], ALU.max), [kmx[h]], [kmx[h]])
        if LEVEL < 4:
            return
        for gi_, gt in enumerate((ga, gb, gi, gf_)):
            b = pin()
            for kc in range(8):
                P.mm(b[0:3, :], winb[:, kc, GOFF + 3 * gi_:GOFF + 3 * gi_ + 3], xmT[:, kc, :], start=(kc == 0), stop=(kc == 7),
                     reads=[winb, xmT], writes=[b])
            evac(gt[:, :], b[0:3, :], [b], [gt])
        if LEVEL < 5:
            return
        for s4 in range(4):
            b = pin()
            for kc in range(8):
                P.mm(b[:, :], xmT[:, kc, s4 * 128:(s4 + 1) * 128], winb[:, kc, TAOFF:TAOFF + 512], start=(kc == 0), stop=(kc == 7),
                     reads=[winb, xmT], writes=[b])
            evac(tma[:, s4, :], b[:, :], [b], [tma])
            b = pin()
            for kc in range(8):
                P.mm(b[:, 0:192], xmT[:, kc, s4 * 128:(s4 + 1) * 128], winb[:, kc, TBOFF:TBOFF + 192], start=(kc == 0), stop=(kc == 7),
                     reads=[winb, xmT], writes=[b])
            evac(tmb[:, s4, :], b[:, 0:192], [b], [tmb])
            for h in range(2):
                G(lambda e, h=h, s4=s4: e.tensor_copy(vxb[h][:, 4 * st + s4, 0:64], tma[:, s4, 64 * h:64 * h + 64]), [tma], [vxb[h]])

    def softplus_neg(out, x):
        V(lambda e: e.scalar_tensor_tensor(out[:], x[:], -1.0, x[:], ALU.mult, ALU.max), [x], [out])
        P.act(out[:], out[:], AF.Exp, scale=-1.0, reads=[out], writes=[out])
        P.act(out[:], out[:], AF.Ln, bias=ones1[0:3, 0:1], reads=[out, ones1], writes=[out])

    def gates(st):
        V(lambda e: e.tensor_scalar(ga[:], ga[:], gpar[:, 1:2], None, ALU.add), [ga, gpar], [ga])
        softplus_neg(gt1, ga)
        V(lambda e: e.scalar_tensor_tensor(gt1[:], ga[:], 0.0, gt1[:], ALU.max, ALU.add), [ga, gt1], [gt1])
        V(lambda e: e.tensor_scalar(gt1[:], gt1[:], nA[:, 0:1], None, ALU.mult), [gt1, nA], [gt1])
        V(lambda e: e.tensor_tensor_scan(Grow[:], rstm[:], gt1[:], 0.0, ALU.mult, ALU.add), [rstm, gt1], [Grow])
        P.act(betar[:], gb[:], AF.Exp, scale=-1.0, reads=[gb], writes=[betar])
        V(lambda e: e.tensor_scalar_add(betar[:], betar[:], 1.0), [betar], [betar])
        V(lambda e: e.reciprocal(betar[:], betar[:]), [betar], [betar])
        P.act(bxg[:], Grow[:], AF.Exp, reads=[Grow], writes=[bxg])
        V(lambda e: e.tensor_tensor(bxg[:], bxg[:], betar[:], ALU.mult), [bxg, betar], [bxg])
        G3 = Grow[:].rearrange("p (a c) -> p a c", a=4)
        V(lambda e: e.tensor_copy(glr[:], Grow[:, 127:512:128]), [Grow], [glr])
        V(lambda e: e.tensor_tensor(edr[:].rearrange("p (a c) -> p a c", a=4), G3, glr[:].unsqueeze(2).to_broadcast([3, 4, 128]),
                                    ALU.subtract), [Grow, glr], [edr])
        P.act(edr[:], edr[:], AF.Exp, scale=-1.0, reads=[edr], writes=[edr])
        P.act(glr[:], glr[:], AF.Exp, reads=[glr], writes=[glr])
        V(lambda e: e.tensor_scalar(gi[:], gi[:], gpar[:, 2:3], None, ALU.add), [gi, gpar], [gi])
        V(lambda e: e.tensor_scalar(gf_[:], gf_[:], gpar[:, 3:4], None, ALU.add), [gf_, gpar], [gf_])
        softplus_neg(gt2, gf_)
        V(lambda e: e.scalar_tensor_tensor(gt2[:], gf_[:], 0.0, gt2[:], ALU.min, ALU.subtract), [gf_, gt2], [gt2])
        V(lambda e: e.tensor_tensor_scan(brow[:], rstm[:], gt2[:], 0.0, ALU.mult, ALU.add), [rstm, gt2], [brow])
        V(lambda e: e.tensor_tensor(apr[:], gi[:], brow[:], ALU.subtract), [gi, brow], [apr])
        V(lambda e: e.tensor_reduce(mloc[:], apr[:].rearrange("p (a c) -> p a c", a=4), AX.X, ALU.max), [apr], [mloc])
        for c in range(4):
            V(lambda e, c=c: e.tensor_tensor(mc[:, c:c + 1], mloc[:, c:c + 1], mprev[:], ALU.max), [mloc, mprev], [mc])
            V(lambda e, c=c: e.tensor_tensor(air[:, c:c + 1], mprev[:], mc[:, c:c + 1], ALU.subtract), [mprev, mc], [air])
            V(lambda e, c=c: e.tensor_tensor(mprev[:], brow[:, 128 * c + 127:128 * c + 128], mc[:, c:c + 1], ALU.add), [brow, mc], [mprev])
        P.act(air[:], air[:], AF.Exp, reads=[air], writes=[air])
        mcb = mc[:].unsqueeze(2).to_broadcast([3, 4, 128])
        V(lambda e: e.tensor_tensor(ws8[:].rearrange("p (a c) -> p a c", a=4), apr[:].rearrange("p (a c) -> p a c", a=4), mcb, ALU.subtract),
          [apr, mc], [ws8])
        P.act(ws8[:], ws8[:], AF.Exp, bias=ln8[0:3, 0:1], reads=[ws8, ln8], writes=[ws8])
        V(lambda e: e.tensor_tensor(thr[:].rearrange("p (a c) -> p a c", a=4), brow[:].rearrange("p (a c) -> p a c", a=4), mcb, ALU.add),
          [brow, mc], [thr])
        P.act(thr[:], thr[:], AF.Exp, scale=-1.0, reads=[thr], writes=[thr])
        rows = (Grow, betar, bxg, edr, ws8, thr)
        for s4 in range(4):
            p_ = ptr[s4]
            for qi, rt in enumerate(rows):
                P.mm(p_[:, 3 * qi:3 * qi + 3], rt[:, s4 * 128:(s4 + 1) * 128], ident[0:3, 0:3], reads=[rt, cst], writes=[p_])
            evac(tmg[:, s4, 0:18], p_[:, 0:18], [p_], [tmg])
        for h in range(3):
            s_ = small()
            P.mm(s_[:, 0:4], sel[:, h, :], glr[:, :], reads=[sel, glr], writes=[s_])
            evac(glc[:, h, :], s_[:, 0:4], [s_], [glc])
            s_ = small()
            P.mm(s_[:, 0:4], sel[:, h, :], air[:, :], reads=[sel, air], writes=[s_])
            evac(aic[:, h, :], s_[:, 0:4], [s_], [aic])

    def k_slot(ch, half):
        for hs, (qc, kc_, hf) in enumerate(SLOT):
            if kc_ == ch and hf == half:
                return hs
        raise KeyError

    def tok_major(st, s4):
        tsl = slice(s4 * 128, (s4 + 1) * 128)
        for ki, ch in enumerate((1, 4, 7)):
            p_ = ptr[ki]
            P.tr(p_[:, 0:128], fm[ch][:, tsl], ident, [fm[ch], cst], [p_])
            for half in range(2):
                hs = k_slot(ch, half)
                cs = slice(64 * half, 64 * half + 64)
                if hs < 3:
                    V(lambda e, ki=ki, p_=p_, cs=cs, hs=hs: e.tensor_scalar(KA[ki][:, cs], p_[:, cs], tmg[:, s4, 6 + hs:7 + hs], None, ALU.mult),
                      [p_, tmg], [KA[ki]])
                    V(lambda e, ki=ki, p_=p_, cs=cs, hs=hs: e.tensor_scalar(KB[ki][:, cs], p_[:, cs], tmg[:, s4, 9 + hs:10 + hs], None, ALU.mult),
                      [p_, tmg], [KB[ki]])
                else:
                    m = hs - 3
                    V(lambda e, ki=ki, p_=p_, cs=cs, m=m: e.tensor_scalar(KA[ki][:, cs], p_[:, cs], tmg[:, s4, 12 + m:13 + m], None, ALU.mult),
                      [p_, tmg], [KA[ki]])
        p_ = ptr[3]
        P.tr(p_[:, 0:128], fm[2][:, tsl], ident, [fm[2], cst], [p_])
        for h in range(2):
            V(lambda e, h=h, p_=p_: e.tensor_scalar(vb[:, 64 * h:64 * h + 64], p_[:, 64 * h:64 * h + 64], tmg[:, s4, 3 + h:4 + h], None, ALU.mult),
              [p_, tmg], [vb])
        P.tr(p_[:, 0:64], fm[5][0:64, tsl], ident[0:64, 0:64], [fm[5], cst], [p_])
        V(lambda e, p_=p_: e.tensor_scalar(vb[:, 128:192], p_[:, 0:64], tmg[:, s4, 5:6], None, ALU.mult), [p_, tmg], [vb])
        G(lambda e: e.tensor_copy(vxm[:, :, 0:64], tma[:, s4, 128:320].rearrange("p (h d) -> p h d", h=3)), [tma], [vxm])

    def gdn_tile(st, s4):
        tsl = slice(s4 * 128, (s4 + 1) * 128)
        mt = mo[st % 2]
        P.act(szz[:], tma[:, s4, 320:512], AF.Exp, scale=-1.0, reads=[tma], writes=[szz])
        G(lambda e: e.tensor_scalar_add(szz[:], szz[:], 1.0), [szz], [szz])
        V(lambda e: e.reciprocal(szz[:], szz[:]), [szz], [szz])
        G(lambda e: e.tensor_tensor(szz[:], szz[:], tma[:, s4, 320:512], ALU.mult), [szz, tma], [szz])
        G(lambda e: e.tensor_tensor(szz[:], szz[:], gnorm[:], ALU.mult), [szz, gnorm], [szz])
        hd = []
        for h in range(3):
            qc, kc_, half = SLOT[h]
            rows = hs_rows(half)
            hd.append((h, fm[qc], fm[kc_], rows, KCH[kc_]))
        for (h, fq, fk, rows, ki) in hd:
            qT = fq[rows, tsl]; kT = fk[rows, tsl]
            Gc = tmg[:, s4, h:h + 1]
            P.mm(gmat[0][:, 0:128], kT, kT, reads=[fk], writes=[gmat[0]])
            P.mm(gmat[1][:, 0:128], kT, qT, reads=[fk, fq], writes=[gmat[1]])
            P.mm(gmat[2][:, 0:128], sel[:, h, :], Grow[:, tsl], start=True, stop=False, reads=[sel, Grow], writes=[gmat[2]])
            P.mm(gmat[2][:, 0:128], ident, mb_incl, start=False, stop=True, reads=[cst], writes=[gmat[2]])
            P.mm(gmat[3][:, 0:128], sel[:, h, :], Grow[:, tsl], start=True, stop=False, reads=[sel, Grow], writes=[gmat[3]])
            P.mm(gmat[3][:, 0:128], ident, mb_strT, start=False, stop=True, reads=[cst], writes=[gmat[3]])
            V(lambda e, h=h, Gc=Gc: e.tensor_scalar(dmt[h][:], gmat[2][:, 0:128], Gc, 0.0, ALU.subtract, ALU.min), [gmat[2], tmg], [dmt[h]])
            P.act(dmt[h][:], dmt[h][:], AF.Exp, reads=[dmt[h]], writes=[dmt[h]])
            V(lambda e, h=h: e.tensor_tensor(attT[h][:], gmat[1][:, 0:128], dmt[h][:], ALU.mult), [gmat[1], dmt[h]], [attT[h]])
            V(lambda e, h=h, Gc=Gc: e.tensor_scalar(dma_[h][:], gmat[3][:, 0:128], Gc, 0.0, ALU.subtract, ALU.max), [gmat[3], tmg], [dma_[h]])
            P.act(dma_[h][:], dma_[h][:], AF.Exp, scale=-1.0, reads=[dma_[h]], writes=[dma_[h]])
            V(lambda e, h=h: e.scalar_tensor_tensor(Am[h][0][:], gmat[0][:, 0:128], tmg[:, s4, 3 + h:4 + h], dma_[h][:], ALU.mult, ALU.mult),
              [gmat[0], tmg, dma_[h]], [Am[h][0]])
            P.tr(chs[2 * h][:, 0:128], Am[h][0][:], ident, [Am[h][0], cst], [chs[2 * h]])
            evac(Bm[h][0][:], chs[2 * h][:, 0:128], [chs[2 * h]], [Bm[h][0]])
            V(lambda e, h=h: e.tensor_tensor(Rm[h][:], ident, Bm[h][0][:], ALU.subtract), [cst, Bm[h][0]], [Rm[h]])
        for k in range(6):
            cur = k % 2; nxt = 1 - cur
            for (h, fq, fk, rows, ki) in hd:
                ca = chs[2 * h]; cb = chs[2 * h + 1]
                P.mm(ca[:, 0:128], Bm[h][cur][:], Am[h][cur][:], reads=[Bm[h][cur], Am[h][cur]], writes=[ca])
                evac(Am[h][nxt][:], ca[:, 0:128], [ca], [Am[h][nxt]])
                if k < 5:
                    P.mm(cb[:, 0:128], Am[h][cur][:], Bm[h][cur][:], reads=[Bm[h][cur], Am[h][cur]], writes=[cb])
                    evac(Bm[h][nxt][:], cb[:, 0:128], [cb], [Bm[h][nxt]])
            for (h, fq, fk, rows, ki) in hd:
                ca = chs[2 * h]
                P.mm(ca[:, 0:128], Am[h][nxt][:], Rm[h][:], reads=[Am[h][nxt], Rm[h]], writes=[ca])
                V(lambda e, h=h, ca=ca: e.tensor_tensor(Rm[h][:], ca[:, 0:128], Rm[h][:], ALU.add), [ca, Rm[h]], [Rm[h]])
        for (h, fq, fk, rows, ki) in hd:
            qT = fq[rows, tsl]
            su = small()
            P.mm(su[:, 0:64], Rm[h][:], vb[:, 64 * h:64 * h + 64], reads=[Rm[h], vb], writes=[su])
            evac(usb[h][:], su[:, 0:64], [su], [usb[h]], eng="act")
            cb = chs[2 * h + 1]
            P.mm(cb[:, 0:128], KA[ki][:], Rm[h][:], reads=[KA[ki], Rm[h]], writes=[cb])
            evac(wTs[h][rows, :], cb[rows, 0:128], [cb], [wTs[h]], eng="act")
            s1 = small()
            P.mm(s1[:, 0:64], wTs[h][rows, :], Sg[h][rows, :], reads=[wTs[h], Sg[h]], writes=[s1])
            V(lambda e, h=h, s1=s1: e.tensor_tensor(vnew[h][:], usb[h][:], s1[:, 0:64], ALU.subtract), [usb[h], s1], [vnew[h]])
            s2 = small()
            P.mm(s2[:, 0:64], qT, Sg[h][rows, :], reads=[fq, Sg[h]], writes=[s2])
            s3 = small()
            P.mm(s3[:, 0:64], attT[h][:], vnew[h][:], reads=[attT[h], vnew[h]], writes=[s3])
            V(lambda e, h=h, s2=s2: e.tensor_scalar(og[h][:], s2[:, 0:64], tmg[:, s4, 18 + h:19 + h], None, ALU.mult), [s2, tmg], [og[h]])
            V(lambda e, h=h, s3=s3: e.tensor_tensor(og[h][:], og[h][:], s3[:, 0:64], ALU.add), [og[h], s3], [og[h]])
            s5 = small()
            P.mm(s5[:, 0:64], KB[ki][:], vnew[h][:], reads=[KB[ki], vnew[h]], writes=[s5])
            V(lambda e, h=h, s5=s5, rows=rows: e.scalar_tensor_tensor(Sg[h][rows, :], Sg[h][rows, :], glc[rows, h, s4:s4 + 1], s5[rows, 0:64],
                                                                     ALU.mult, ALU.add), [Sg[h], glc, s5], [Sg[h]])
            V(lambda e, h=h: e.bn_stats(st6[:], og[h][:]), [og[h]], [st6])
            V(lambda e: e.bn_aggr(mv2[:], st6[:]), [st6], [mv2])
            V(lambda e: e.scalar_tensor_tensor(sm1[:, 0:1], mv2[:, 0:1], mv2[:, 0:1], mv2[:, 1:2], ALU.mult, ALU.add), [mv2], [sm1])
            P.act(sm1[:, 0:1], sm1[:, 0:1], AF.Ln, bias=epsn[:, 0:1], reads=[sm1, epsn], writes=[sm1])
            P.act(sm1[:, 0:1], sm1[:, 0:1], AF.Exp, scale=-0.5, reads=[sm1], writes=[sm1])
            V(lambda e, h=h: e.scalar_tensor_tensor(mt[:, s4, 64 * h:64 * h + 64], og[h][:], sm1[:, 0:1], szz[:, 64 * h:64 * h + 64],
                                                   ALU.mult, ALU.mult), [og[h], sm1, szz], [mt])

    def mlstm_tile(st, s4):
        tsl = slice(s4 * 128, (s4 + 1) * 128)
        mt = mo[st % 2]
        for m in range(3):
            qc, kc_, half = SLOT[3 + m]
            rows = hs_rows(half); ki = KCH[kc_]
            qT = fm[qc][rows, tsl]; kT = fm[kc_][rows, tsl]
            gk = gmat[m]
            P.mm(gk[:, 0:128], kT, qT, reads=[fm[qc], fm[kc_]], writes=[gk])
            V(lambda e, m=m, gk=gk: e.scalar_tensor_tensor(PTm[m][:], gk[:, 0:128], tmg[:, s4, 12 + m:13 + m], maskU, ALU.mult, ALU.mult),
              [gk, tmg, cst], [PTm[m]])
            V(lambda e, m=m, rows=rows: e.tensor_scalar(Cm[m][rows, :], Cm[m][rows, :], aic[rows, m, s4:s4 + 1], None, ALU.mult), [Cm[m], aic], [Cm[m]])
            sA = small()
            P.mm(sA[:, 0:65], qT, Cm[m][rows, :], reads=[fm[qc], Cm[m]], writes=[sA])
            evac(nda[m][:], sA[:, 0:65], [sA], [nda[m]], eng="act")
            sB = small()
            P.mm(sB[:, 0:65], PTm[m][:], vxm[:, m, :], reads=[PTm[m], vxm], writes=[sB])
            V(lambda e, m=m, sB=sB: e.tensor_tensor(nda[m][:], nda[m][:], sB[:, 0:65], ALU.add), [nda[m], sB], [nda[m]])
            sC = small()
            P.mm(sC[:, 0:65], KA[ki][:], vxm[:, m, :], reads=[KA[ki], vxm], writes=[sC])
            V(lambda e, m=m, sC=sC, rows=rows: e.tensor_tensor(Cm[m][rows, :], Cm[m][rows, :], sC[rows, 0:65], ALU.add), [Cm[m], sC], [Cm[m]])
            V(lambda e, m=m: e.scalar_tensor_tensor(sm1[:, 1:2], nda[m][:, 64:65], -1.0, nda[m][:, 64:65], ALU.mult, ALU.max), [nda[m]], [sm1])
            V(lambda e, m=m: e.tensor_tensor(sm1[:, 1:2], sm1[:, 1:2], tmg[:, s4, 15 + m:16 + m], ALU.max), [sm1, tmg], [sm1])
            V(lambda e: e.reciprocal(sm1[:, 1:2], sm1[:, 1:2]), [sm1], [sm1])
            V(lambda e, m=m: e.tensor_scalar(hm[:, 64 * m:64 * m + 64], nda[m][:, 0:64], sm1[:, 1:2], None, ALU.mult), [nda[m], sm1], [hm])
        P.act(sgo[:], tmb[:, s4, :], AF.Exp, scale=-1.0, reads=[tmb], writes=[sgo])
        G(lambda e: e.tensor_scalar_add(sgo[:], sgo[:], 1.0), [sgo], [sgo])
        V(lambda e: e.reciprocal(sgo[:], sgo[:]), [sgo], [sgo])
        V(lambda e: e.tensor_tensor(hm[:], hm[:], sgo[:], ALU.mult), [hm, sgo], [hm])
        for m in range(3):
            cs = slice(64 * m, 64 * m + 64)
            V(lambda e, cs=cs: e.bn_stats(st6[:], hm[:, cs]), [hm], [st6])
            V(lambda e: e.bn_aggr(mv2[:], st6[:]), [st6], [mv2])
            P.act(mv2[:, 1:2], mv2[:, 1:2], AF.Ln, bias=epsn[:, 0:1], reads=[mv2, epsn], writes=[mv2])
            P.act(mv2[:, 1:2], mv2[:, 1:2], AF.Exp, scale=-0.5, reads=[mv2], writes=[mv2])
            V(lambda e, cs=cs: e.tensor_scalar(hm[:, cs], hm[:, cs], mv2[:, 0:1], mv2[:, 1:2], ALU.subtract, ALU.mult), [hm, mv2], [hm])
            G(lambda e, m=m, cs=cs: e.tensor_tensor(mt[:, s4, 320 + 64 * m:384 + 64 * m], hm[:, cs], mnorm[:, cs], ALU.mult), [hm, mnorm], [mt])

    ptrr = [0]

    def moba(st):
        mt = mo[st % 2]
        for h in range(2):
            sk = small()
            P.mm(sk[:, 0:1], onesrow[0:1, :], kmx[h][0:1, 0:1], reads=[onesrow, kmx[h]], writes=[sk])
            V(lambda e, sk=sk: e.tensor_scalar_mul(kmxc[:], sk[:, 0:1], 1.0 / 64), [sk], [kmxc])
            P.act(sq[0:64, :], qf[h][:, :], AF.Square, reads=[qf[h]], writes=[sq])
            for s4 in range(4):
                tsl = slice(s4 * 128, (s4 + 1) * 128)
                qb = 4 * st + s4; own = qb // 2
                s0 = small()
                P.mm(s0[:, 0:1], sq[0:64, tsl], ones1[0:64, 0:1], reads=[sq, ones1], writes=[s0])
                P.act(negm[:, 0:1], s0[:, 0:1], AF.Ln, scale=kmxc[:, 0:1], bias=epsn[:, 0:1], reads=[s0, kmxc, epsn], writes=[negm])
                P.act(negm[:, 0:1], negm[:, 0:1], AF.Exp, scale=0.5, reads=[negm], writes=[negm])
                V(lambda e: e.tensor_scalar(negm[:, 1:2], negm[:, 0:1], -1.0, -BIGM, ALU.mult, ALU.add), [negm], [negm])
                V(lambda e: e.tensor_scalar_mul(negm[:, 0:1], negm[:, 0:1], -1.0), [negm], [negm])
                sg_ = small()
                P.mm(sg_[:, 0:32], qf[h][:, tsl], kmean[h][:, :], reads=[qf[h], kmean[h]], writes=[sg_])
                V(lambda e, sg_=sg_, own=own: e.tensor_tensor(gm[:], sg_[:, 0:32], wm[:, 32 - own:64 - own], ALU.add), [sg_, wm], [gm])
                V(lambda e: e.max(top8[:], gm[:]), [gm], [top8])
                V(lambda e: e.tensor_scalar_max(sm1[:, 2:3], top8[:, 2:3], -1.0e8), [top8], [sm1])
                V(lambda e: e.tensor_scalar(gm[:], gm[:], sm1[:, 2:3], BIGM, ALU.is_ge, ALU.mult), [gm, sm1], [gm])
                V(lambda e: e.tensor_scalar(sbt[:, 64:96], gm[:], negm[:, 1:2], None, ALU.add), [gm, negm], [sbt])
                V(lambda e, own=own: e.tensor_copy(sbt[:, 64 + own:65 + own], negm[:, 0:1]), [negm], [sbt])
                p_ = ptr[ptrr[0] % 4]; ptrr[0] += 1
                P.tr(p_[0:96, 0:128], sbt[:, 0:96], ident, [sbt, cst], [p_])
                V(lambda e, h=h, p_=p_, tsl=tsl: e.tensor_copy(qext[h][64:96, tsl], p_[64:96, 0:128]), [p_], [qext[h]])
            nkt = 4 * st + 4
            for kt in range(nkt):
                i0 = max(0, kt - 4 * st)
                n = 512 - 128 * i0
                P.mm(bk[6][:, 0:n], kext[h][0:96, kt * 128:(kt + 1) * 128], qext[h][0:96, i0 * 128:512],
                     reads=[kext[h], qext[h]], writes=[bk[6]])
                pt = PT[kt % 2]
                P.act(pt[:, 0:n], bk[6][:, 0:n], AF.Exp, reads=[bk[6]], writes=[pt])
                if kt >= 4 * st:
                    G(lambda e, pt=pt: e.tensor_tensor(pt[:, 0:128], pt[:, 0:128], maskUb[:], ALU.mult), [pt, maskUb], [pt])
                for i in range(i0, 4):
                    j = i - i0
                    P.op("pe", lambda e, i=i, j=j, pt=pt, kt=kt, h=h: e.matmul(
                        pacc[i][:, 0:65], pt[:, j * 128:(j + 1) * 128], vxb[h][:, kt, :],
                        start=(kt == 0 and i == 0), stop=(kt == 4 * st + i), skip_group_check=True),
                        [pt, vxb[h]], [pacc[i]])
            for i in range(4):
                V(lambda e, i=i: e.reciprocal(sm1[:, 3:4], pacc[i][:, 64:65]), [pacc[i]], [sm1])
                V(lambda e, i=i, h=h: e.tensor_scalar(mt[:, i, 192 + 64 * h:256 + 64 * h], pacc[i][:, 0:64], sm1[:, 3:4], None, ALU.mult),
                  [pacc[i], sm1], [mt])

    for h in range(2):
        G(lambda e, h=h: e.memset(kmean[h][:], 0.0), [], [kmean[h]])
    G(lambda e: e.memset(mo[0][:], 0.0), [], [mo[0]])
    G(lambda e: e.memset(tmg[:], 0.0), [], [tmg])
    load_x(0, 0)
    for st in range(nst):
        if LEVEL >= 0:
            inproj(st)
        if "gates" in STAGES:
            gates(st)
            P.act(tmg[:, :, 18:21], tmg[:, :, 0:3], AF.Exp, reads=[tmg], writes=[tmg])
        for s4 in range(4):
            if "tok" in STAGES:
                tok_major(st, s4)
            if "gdn" in STAGES:
                gdn_tile(st, s4)
            if "mlstm" in STAGES:
                mlstm_tile(st, s4)
        if "moba" in STAGES:
            moba(st)
        P.dma(mix[st * 512:(st + 1) * 512, :].rearrange("(s p) c -> p s c", p=128), mo[st % 2][:], reads=[mo[st % 2]], writes=[mix])
    P.emit()
    return nc


def _mixer_consts():
    idn = np.eye(128, dtype=np.float32)
    s_ = np.arange(128)[:, None]; l_ = np.arange(128)[None, :]
    mb_incl = np.where(l_ >= s_, 0.0, -BIGM).astype(np.float32)
    mb_strT = np.where(l_ < s_, 0.0, BIGM).astype(np.float32)
    maskU = (s_ <= l_).astype(np.float32)
    bd = np.zeros((128, 128), np.float32); bd[:64, :64] = 1; bd[64:, 64:] = 1
    cst = np.stack([idn, mb_incl, mb_strT, maskU, bd, np.zeros((128, 128), np.float32)], axis=1)
    sel = np.zeros((3, 3, 128), np.float32)
    for h in range(3):
        sel[h, h, :] = 1
    rstm = np.ones((3, 512), np.float32); rstm[:, ::128] = 0
    oh = np.zeros((32, T), np.float32)
    for n in range(32):
        oh[n, n * 256:(n + 1) * 256] = 1
    wmask = np.concatenate([np.zeros(32, np.float32), np.full(32, -1.0e9, np.float32)])[None, :]
    return {"cst": np.ascontiguousarray(cst), "sel": sel, "rstm": rstm, "onehot": oh, "wmask": wmask}


def _mixer_cols(hh):
    g = [3 * hh + i for i in range(3)]
    mh = [2 * hh + i for i in range(2)]
    lh = [3 * hh + i for i in range(3)]
    r64 = lambda base, h: list(range(base + 64 * h, base + 64 * h + 64))
    gq = lambda h: r64(0, h); gk = lambda h: r64(384, h); gv = lambda h: r64(768, h)
    cq = lambda h: r64(2316, h); ck = lambda h: r64(2700, h)
    chunks = [gq(g[0]) + gq(g[1]), gk(g[0]) + gk(g[1]), gv(g[0]) + gv(g[1]),
              gq(g[2]) + cq(lh[0]), gk(g[2]) + ck(lh[0]), gv(g[2]),
              cq(lh[1]) + cq(lh[2]), ck(lh[1]) + ck(lh[2])]
    cols = [c for ch in chunks for c in ch]
    for h in mh:
        cols += r64(1548, h) + r64(1804, h)
    cols += [1152 + h for h in g] + [1158 + h for h in g] + [3468 + h for h in lh] + [3474 + h for h in lh]
    for h in mh:
        cols += r64(2060, h)
    for h in lh:
        cols += r64(3084, h)
    for h in g:
        cols += r64(1164, h)
    for h in lh:
        cols += r64(3480, h)
    assert len(cols) == 1932
    return np.array(cols), chunks, g, mh, lh


def run_mixer(l, x1, mod1, inp):
    nc = _prog("mixer", build_mixer_prog)
    consts = _mixer_consts()
    maps = []
    for k in range(8):
        b = k // 2; hh = k % 2
        cols, chunks, g, mh, lh = _mixer_cols(hh)
        convw = np.zeros((128, 8, 4), np.float32)
        for i, ch in enumerate(chunks):
            for p, c in enumerate(ch):
                if c < 1152:
                    convw[p, i, :] = inp["gdn_conv"][l][:, c]
                else:
                    convw[p, i, :] = inp["mlstm_conv"][l][:, c - 2316]
        gpar = np.stack([inp["gdn_a_log"][l][g], inp["gdn_dt_bias"][l][g], inp["mlstm_i_bias"][l][lh], inp["mlstm_f_bias"][l][lh]], axis=1)
        m = {"x1": x1[b * T:(b + 1) * T],
             "modc": np.ascontiguousarray(np.concatenate([_cols(mod1[b, 1]), _cols(mod1[b, 0])], axis=1)),
             "win": np.ascontiguousarray(inp["w_in"][l][:, cols]),
             "convw": convw, "gpar": np.ascontiguousarray(gpar.astype(np.float32)),
             "gnorm": np.ascontiguousarray(np.tile(inp["gdn_norm"][l], 3)[None, :]),
             "mnorm": np.ascontiguousarray(np.concatenate([inp["mlstm_norm"][l][64 * h:64 * h + 64] for h in lh])[None, :])}
        m.update(consts)
        maps.append(m)
    r = _run(nc, maps)
    out = np.zeros((NB * T, D), np.float32)
    for k in range(8):
        b = k // 2; hh = k % 2
        mk = r[k]["mix"]
        rs = slice(b * T, (b + 1) * T)
        out[rs, 192 * hh:192 * hh + 192] = mk[:, 0:192]
        out[rs, 384 + 128 * hh:384 + 128 * hh + 128] = mk[:, 192:320]
        out[rs, 640 + 192 * hh:640 + 192 * hh + 192] = mk[:, 320:512]
    return out


def kernel(x, c, ada_w, ada_b, ffn_w13, ffn_w2, w_in, w_out, gdn_conv, gdn_a_log, gdn_dt_bias,
           gdn_norm, mlstm_conv, mlstm_i_bias, mlstm_f_bias, mlstm_norm, ln_g, ln_b):
    f = lambda a: np.ascontiguousarray(np.asarray(a, dtype=np.float32))
    inp = {"w_in": f(w_in), "gdn_conv": f(gdn_conv), "gdn_a_log": f(gdn_a_log), "gdn_dt_bias": f(gdn_dt_bias),
           "gdn_norm": f(gdn_norm), "mlstm_conv": f(mlstm_conv), "mlstm_i_bias": f(mlstm_i_bias),
           "mlstm_f_bias": f(mlstm_f_bias), "mlstm_norm": f(mlstm_norm)}
    ffn_w13 = f(ffn_w13); ffn_w2 = f(ffn_w2); w_out = f(w_out); ln_g = f(ln_g); ln_b = f(ln_b)
    mod = run_mod(f(c), f(ada_w), f(ada_b))
    xs = f(x).reshape(NB * T, D)

    def ffn_pp(l, i, sub):
        return {"mod": mod[l, :, sub], "lng": ln_g[l, sub], "lnb": ln_b[l, sub], "w13": ffn_w13[l, i], "w2": ffn_w2[l, i]}

    def lin_pp(l):
        return {"mod": mod[l, :, 1], "lng": ln_g[l, 1], "lnb": ln_b[l, 1], "wout": w_out[l]}

    x1 = run_dense(["ffn"], xs, None, [ffn_pp(0, 0, 0)])
    mix0 = run_mixer(0, x1, mod[0, :, 1], inp)
    x1b = run_dense(["lin", "ffn", "ffn"], x1, mix0, [lin_pp(0), ffn_pp(0, 1, 2), ffn_pp(1, 0, 0)])
    mix1 = run_mixer(1, x1b, mod[1, :, 1], inp)
    out = run_dense(["lin", "ffn"], x1b, mix1, [lin_pp(1), ffn_pp(1, 1, 2)])
    return out.reshape(NB, T, D).astype(np.float32)
```
